# Optimizing a Trainium2 kernel written in Bass

```python
import jax, jax.numpy as jnp
from jax import lax
import numpy as np

D_MODEL = 1024
BATCH = 16
SEQ = 4096
DEPTH = 2

CHUNK = 64
Q_BLOCK = 128
FOX_HEADS = 6
FOX_HEAD_DIM = 64
FOX_WIDTH = FOX_HEADS * FOX_HEAD_DIM
FOX_FORGET_BIAS_MEAN = 3.0
HGRN_HEADS = 4
HGRN_KEY_DIM = 128
HGRN_VAL_DIM = 96
HGRN_KEY_WIDTH = HGRN_HEADS * HGRN_KEY_DIM
HGRN_VAL_WIDTH = HGRN_HEADS * HGRN_VAL_DIM
CONV_CH = 256
CONV_WIDTH = 31
N_BRANCH = 3
BRANCH_WIDTH = FOX_WIDTH + HGRN_VAL_WIDTH + CONV_CH
D_FF = 2816
FFN_CONV_WIDTH = 3
EPS = 1e-6

IN_SIZES = (FOX_WIDTH, FOX_WIDTH, FOX_WIDTH, FOX_HEADS,
            HGRN_KEY_WIDTH, HGRN_KEY_WIDTH, HGRN_VAL_WIDTH, HGRN_VAL_WIDTH,
            CONV_CH, CONV_CH, N_BRANCH * D_MODEL)
IN_WIDTH = sum(IN_SIZES)
IN_SPLITS = tuple(int(v) for v in np.cumsum(IN_SIZES)[:-1])

kernel_name = "hybrid_fox_hgrn2_conformer_gated_trunk"


def rmsnorm(x, g):
    x32 = x.astype(jnp.float32)
    y = x32 * lax.rsqrt(jnp.mean(x32 * x32, axis=-1, keepdims=True) + EPS)
    return (y * g.astype(jnp.float32)).astype(x.dtype)


def layernorm(x, g, b):
    x32 = x.astype(jnp.float32)
    mu = jnp.mean(x32, axis=-1, keepdims=True)
    xc = x32 - mu
    y = xc * lax.rsqrt(jnp.mean(xc * xc, axis=-1, keepdims=True) + EPS)
    return (y * g.astype(jnp.float32) + b.astype(jnp.float32)).astype(x.dtype)


def causal_dwconv(u, w, b):
    k = w.shape[0]
    y = lax.conv_general_dilated(
        u, w[:, None, :].astype(u.dtype), window_strides=(1,), padding=[(k - 1, 0)],
        dimension_numbers=("NWC", "WIO", "NWC"), feature_group_count=u.shape[-1])
    return y + b.astype(u.dtype)


def split_heads(t, n_heads):
    b, s, w = t.shape
    return t.reshape(b, s, n_heads, w // n_heads).transpose(0, 2, 1, 3)


def fox_attention(q, k, v, logf):
    s_len = q.shape[2]
    scale = q.shape[-1] ** -0.5
    c = jnp.cumsum(logf, axis=-1)
    outs = []
    for start in range(0, s_len, Q_BLOCK):
        end = start + Q_BLOCK
        qb = q[:, :, start:end]
        kb = k[:, :, :end]
        vb = v[:, :, :end]
        logits = jnp.einsum("bhqd,bhkd->bhqk", qb, kb).astype(jnp.float32) * scale
        logits = logits + (c[:, :, start:end, None] - c[:, :, None, :end])
        mask = (start + jnp.arange(Q_BLOCK))[:, None] >= jnp.arange(end)[None, :]
        logits = jnp.where(mask, logits, -jnp.inf)
        p = jax.nn.softmax(logits, axis=-1)
        outs.append(jnp.einsum("bhqk,bhkd->bhqd", p.astype(v.dtype), vb))
    return jnp.concatenate(outs, axis=2)


def hgrn2_chunkwise(q, k, v, logf):
    b, h, s_len, kd = q.shape
    vd = v.shape[-1]
    n = s_len // CHUNK
    def to_chunks(t):
        t = t.astype(jnp.float32).reshape(b, h, n, CHUNK, t.shape[-1])
        return jnp.moveaxis(t, 2, 0)
    qc, kc, vc, lc = to_chunks(q), to_chunks(k), to_chunks(v), to_chunks(logf)
    bc = jnp.cumsum(lc, axis=-2)
    tri = jnp.tril(jnp.ones((CHUNK, CHUNK), dtype=bool))

    def step(state, inp):
        qx, kx, vx, bx = inp
        diff = bx[:, :, :, None, :] - bx[:, :, None, :, :]
        decay = jnp.exp(jnp.where(tri[:, :, None], diff, -jnp.inf))
        attn = jnp.einsum("bhtk,bhsk,bhtsk->bhts", qx, kx, decay)
        o = (jnp.einsum("bhts,bhsv->bhtv", attn, vx)
             + jnp.einsum("bhtk,bhkv->bhtv", qx * jnp.exp(bx), state))
        b_last = bx[:, :, -1:, :]
        new_state = (jnp.exp(b_last[:, :, 0, :])[..., None] * state
                     + jnp.einsum("bhsk,bhsv->bhkv", kx * jnp.exp(b_last - bx), vx))
        return new_state, o

    state0 = jnp.zeros((b, h, kd, vd), jnp.float32)
    _, ys = lax.scan(step, state0, (qc, kc, vc, bc))
    ys = jnp.moveaxis(ys, 0, 2).reshape(b, h, s_len, vd)
    return ys.astype(v.dtype)


def setup_inputs(seed: int = 0) -> dict:
    key = jax.random.key(seed)
    ks = jax.random.split(key, 21)
    L, D = DEPTH, D_MODEL
    def nrm(k, shape, scale):
        return jax.random.normal(k, shape, jnp.float32) * scale
    return {
        "x": nrm(ks[0], (BATCH, SEQ, D), 1.0),
        "norm_mix_g": 1.0 + nrm(ks[1], (L, D), 0.02),
        "w_in": nrm(ks[2], (L, D, IN_WIDTH), D ** -0.5),
        "fox_forget_b": FOX_FORGET_BIAS_MEAN + nrm(ks[3], (L, FOX_HEADS), 0.1),
        "fox_q_norm_g": 1.0 + nrm(ks[4], (L, FOX_HEAD_DIM), 0.02),
        "fox_k_norm_g": 1.0 + nrm(ks[5], (L, FOX_HEAD_DIM), 0.02),
        "hgrn_lb_logits": nrm(ks[6], (L, HGRN_KEY_WIDTH), 0.5),
        "hgrn_out_norm_g": 1.0 + nrm(ks[7], (L, HGRN_VAL_DIM), 0.02),
        "conv_dw_w": nrm(ks[8], (L, CONV_WIDTH, CONV_CH), CONV_WIDTH ** -0.5),
        "conv_dw_b": nrm(ks[9], (L, CONV_CH), 0.02),
        "conv_norm_g": 1.0 + nrm(ks[10], (L, CONV_CH), 0.02),
        "conv_norm_b": nrm(ks[11], (L, CONV_CH), 0.02),
        "gate_b": nrm(ks[12], (L, N_BRANCH * D), 0.1),
        "w_branch": nrm(ks[13], (L, BRANCH_WIDTH, D), FOX_WIDTH ** -0.5),
        "w_out": nrm(ks[14], (L, D, D), D ** -0.5),
        "norm_ffn_g": 1.0 + nrm(ks[15], (L, D), 0.02),
        "w_up": nrm(ks[16], (L, D, 2 * D_FF), D ** -0.5),
        "ffn_dw_w": nrm(ks[17], (L, FFN_CONV_WIDTH, D_FF), FFN_CONV_WIDTH ** -0.5),
        "ffn_dw_b": nrm(ks[18], (L, D_FF), 0.02),
        "w_down": nrm(ks[19], (L, D_FF, D), D_FF ** -0.5),
    }


def reference(x, norm_mix_g, w_in, fox_forget_b, fox_q_norm_g, fox_k_norm_g,
              hgrn_lb_logits, hgrn_out_norm_g, conv_dw_w, conv_dw_b, conv_norm_g,
              conv_norm_b, gate_b, w_branch, w_out, norm_ffn_g, w_up, ffn_dw_w,
              ffn_dw_b, w_down):
    b, s_len, d = x.shape
    lb_all = jnp.cumsum(jax.nn.softmax(hgrn_lb_logits.astype(jnp.float32), axis=0), axis=0)
    lb_all = jnp.maximum(lb_all - lb_all[0:1], 0.0)
    o_a = FOX_WIDTH
    o_b = FOX_WIDTH + HGRN_VAL_WIDTH
    for l in range(DEPTH):
        h = rmsnorm(x, norm_mix_g[l])
        z = h @ w_in[l]
        (fq, fk, fv, ff, hq, hf, hi, hg, ca, cb, gl) = jnp.split(z, IN_SPLITS, axis=-1)

        qa = rmsnorm(split_heads(fq, FOX_HEADS), fox_q_norm_g[l])
        ka = rmsnorm(split_heads(fk, FOX_HEADS), fox_k_norm_g[l])
        va = split_heads(fv, FOX_HEADS)
        logf_a = jax.nn.log_sigmoid(ff.astype(jnp.float32) + fox_forget_b[l]).transpose(0, 2, 1)
        ya = fox_attention(qa, ka, va, logf_a)
        ya = ya.transpose(0, 2, 1, 3).reshape(b, s_len, FOX_WIDTH)

        lb = lb_all[l]
        logf_b = jnp.logaddexp(jnp.log(lb), jnp.log1p(-lb) + jax.nn.log_sigmoid(hf.astype(jnp.float32)))
        kb_in = -jnp.expm1(logf_b)
        yb = hgrn2_chunkwise(split_heads(hq, HGRN_HEADS),
                             split_heads(kb_in.astype(x.dtype), HGRN_HEADS),
                             split_heads(hi, HGRN_HEADS),
                             split_heads(logf_b, HGRN_HEADS))
        yb = rmsnorm(yb.transpose(0, 2, 1, 3), hgrn_out_norm_g[l])
        yb = yb * jax.nn.silu(hg.reshape(b, s_len, HGRN_HEADS, HGRN_VAL_DIM))
        yb = yb.reshape(b, s_len, HGRN_VAL_WIDTH)

        yc = ca * jax.nn.sigmoid(cb)
        yc = causal_dwconv(yc, conv_dw_w[l], conv_dw_b[l])
        yc = jax.nn.silu(layernorm(yc, conv_norm_g[l], conv_norm_b[l]))

        gates = jax.nn.sigmoid((gl + gate_b[l]).astype(jnp.float32)).astype(x.dtype)
        gates = gates.reshape(b, s_len, N_BRANCH, d)
        wb = w_branch[l]
        merged = (gates[:, :, 0] * (ya @ wb[:o_a])
                  + gates[:, :, 1] * (yb @ wb[o_a:o_b])
                  + gates[:, :, 2] * (yc @ wb[o_b:]))
        x = x + merged @ w_out[l]

        h2 = rmsnorm(x, norm_ffn_g[l])
        gate_in, value_in = jnp.split(h2 @ w_up[l], 2, axis=-1)
        gate_in = causal_dwconv(gate_in, ffn_dw_w[l], ffn_dw_b[l])
        x = x + (jax.nn.silu(gate_in) * value_in) @ w_down[l]
    return x
```

```python
import contextlib
import numpy as np
import concourse.bass as bass
import concourse.mybir as mybir
from concourse.bass_utils import run_bass_kernel_spmd

F32 = mybir.dt.float32
BF16 = mybir.dt.bfloat16
AF = mybir.ActivationFunctionType
ALU = mybir.AluOpType
AX = mybir.AxisListType

D = 1024
FH, FD, FW = 6, 64, 384
HH, HK, HV = 4, 128, 96
CC, CK = 256, 31
DFF = 2816
NFF = DFF // 128
EPS = 1e-6
INW = 6534
T = 512
NCH = 35
NRING = 5

O_FQ, O_FK, O_FV, O_FF = 0, 384, 768, 1152
O_HQ, O_HF, O_HI, O_HG = 1158, 1670, 2182, 2566
O_CA, O_CB, O_GL = 2950, 3206, 3462


def chunk_table():
    t = []
    t.append(("w_in", 0, 8, [(0, 512, 0)]))
    t.append(("w_in", 0, 8, [(512, 512, 0)]))
    t.append(("w_in", 0, 8, [(1024, 134, 0)]))
    t.append(("w_in", 0, 8, [(O_HQ, 512, 0)]))
    t.append(("w_in", 0, 8, [(O_HF, 512, 0)]))
    t.append(("w_in", 0, 8, [(O_HI, 384, 0)]))
    t.append(("w_in", 0, 8, [(O_HG, 384, 0)]))
    t.append(("w_in", 0, 8, [(O_CA, 512, 0)]))
    for i in range(6):
        t.append(("w_in", 0, 8, [(O_GL + 512 * i, 512, 0)]))
    for i in range(2):
        t.append(("w_branch", 0, 8, [(512 * i, 512, 0)]))
    for i in range(2):
        t.append(("w_out", 0, 8, [(512 * i, 512, 0)]))
    for i in range(11):
        t.append(("w_up", 0, 8, [(256 * i, 256, 0), (DFF + 256 * i, 256, 256)]))
    for hf in range(2):
        for kg in range(3):
            nk = 8 if kg < 2 else 6
            t.append(("w_down", 1024 * kg, nk, [(512 * hf, 512, 0)]))
    assert len(t) == NCH
    return t


CH_A, CH_HQ, CH_HF, CH_HI, CH_HG, CH_C, CH_G, CH_BR, CH_OUT, CH_UP, CH_DN = 0, 3, 4, 5, 6, 7, 8, 14, 16, 18, 29
ORDER = [0, 1, 2, 5, 6, 3, 4, 7] + list(range(8, 14)) + [14, 15, 16, 17] + list(range(18, 29)) + list(range(29, 35))


class Prog:
    def __init__(self, nc, es):
        self.nc = nc
        self.es = es
        self.eng = {"pe": nc.tensor, "act": nc.scalar, "dve": nc.vector, "pool": nc.gpsimd, "sp": nc.sync}
        self.sem = {k: es.enter_context(nc.semaphore("sem_" + k)) for k in self.eng}
        self.cnt = {k: 0 for k in self.eng}
        self.waited = {k: {} for k in self.eng}
        self.lw = {}
        self.rd = {}
        self.dsem = {}
        self.dcnt = {}
        self.nwaits = 0

    BANKED = {"big": (24 * 512, 512)}

    def res(self, ap):
        n = ap.tensor.name
        if n in self.BANKED:
            pstride, bank = self.BANKED[n]
            dims = list(ap.ap)
            lo = int(ap.offset) % pstride
            hi = lo + sum((c - 1) * st for st, c in dims[1:])
            return [f"{n}{i}" for i in range(lo // bank, hi // bank + 1)]
        return [n]

    def dma_sem(self, name):
        if name not in self.dsem:
            self.dsem[name] = self.es.enter_context(self.nc.semaphore("dsem_" + name))
            self.dcnt[name] = 0
        return name

    def _semobj(self, key):
        return self.sem[key[1]] if key[0] == "e" else self.dsem[key[1]]

    def _sync(self, eng, reads, writes):
        deps = {}

        def add(tok):
            if tok is None:
                return
            k, v = tok
            if deps.get(k, 0) < v:
                deps[k] = v

        for r in reads:
            add(self.lw.get(r))
        for w in writes:
            add(self.lw.get(w))
            for k, v in self.rd.get(w, {}).items():
                add((k, v))
        for k, v in deps.items():
            if k == ("e", "pe") and eng == "pe":
                continue
            if self.waited[eng].get(k, 0) < v:
                self.eng[eng].wait_ge(self._semobj(k), v)
                self.waited[eng][k] = v
                self.nwaits += 1

    def _commit(self, tok, reads, writes):
        k, v = tok
        for r in reads:
            d = self.rd.setdefault(r, {})
            if d.get(k, 0) < v:
                d[k] = v
        for w in writes:
            self.lw[w] = tok
            self.rd[w] = {}

    def op(self, eng, fn, ins, outs, r=None, w=None):
        reads = list(r) if r is not None else [x for a in ins for x in self.res(a)]
        writes = list(w) if w is not None else [x for a in outs for x in self.res(a)]
        self._sync(eng, reads, writes)
        i = fn(self.eng[eng])
        self.cnt[eng] += 1
        i.then_inc(self.sem[eng], 1)
        self._commit((("e", eng), self.cnt[eng]), reads, writes)

    def dma(self, q, out, in_, semname, r=None, w=None, **kw):
        reads = list(r) if r is not None else self.res(in_)
        writes = list(w) if w is not None else self.res(out)
        self.dma_sem(semname)
        self._sync(q, reads, writes)
        i = self.eng[q].dma_start(out=out, in_=in_, **kw)
        self.dcnt[semname] += 16
        i.then_inc(self.dsem[semname], 16)
        self._commit((("d", semname), self.dcnt[semname]), reads, writes)

    def mm(self, out, lhsT, rhs, start=True, stop=True, r=None, w=None, skip=False):
        if skip:
            self.op("pe", lambda e: e.matmul(out, lhsT, rhs, start=start, stop=stop, skip_group_check=True), [lhsT, rhs], [out], r, w)
        else:
            self.op("pe", lambda e: e.matmul(out, lhsT, rhs, start=start, stop=stop), [lhsT, rhs], [out], r, w)

    def tr(self, out, in_, ident, r=None, w=None):
        self.op("pe", lambda e: e.transpose(out, in_, ident), [in_, ident], [out], r, w)

    def act(self, out, in_, func, bias=None, scale=None, eng="act", r=None, w=None):
        kw = {}
        ins = [in_]
        if bias is not None:
            kw["bias"] = bias
            if not isinstance(bias, (int, float)):
                ins.append(bias)
        if scale is not None:
            kw["scale"] = scale
            if not isinstance(scale, (int, float)):
                ins.append(scale)
        self.op("act", lambda e: e.activation(out, in_, func, **kw), ins, [out], r, w)

    def tt(self, out, in0, in1, op, eng="dve", r=None, w=None):
        self.op(eng, lambda e: e.tensor_tensor(out, in0, in1, op), [in0, in1], [out], r, w)

    def ts(self, out, in0, s1, s2, op0, op1=None, eng="dve", r=None, w=None):
        ins = [in0] + [s for s in (s1, s2) if s is not None and not isinstance(s, (int, float))]
        if op1 is None:
            self.op(eng, lambda e: e.tensor_scalar(out, in0, s1, None, op0), ins, [out], r, w)
        else:
            self.op(eng, lambda e: e.tensor_scalar(out, in0, s1, s2, op0, op1), ins, [out], r, w)

    def stt(self, out, in0, scalar, in1, op0, op1, r=None, w=None):
        ins = [in0, in1] + ([] if isinstance(scalar, (int, float)) else [scalar])
        self.op("dve", lambda e: e.scalar_tensor_tensor(out, in0, scalar, in1, op0, op1), ins, [out], r, w)

    def copy(self, out, in_, eng="dve", r=None, w=None):
        if eng == "act":
            self.op("act", lambda e: e.copy(out, in_), [in_], [out], r, w)
        else:
            self.op(eng, lambda e: e.tensor_copy(out, in_), [in_], [out], r, w)

    def red(self, out, in_, op=ALU.add, r=None, w=None):
        self.op("dve", lambda e: e.tensor_reduce(out, in_, AX.X, op), [in_], [out], r, w)

    def recip(self, out, in_, r=None, w=None):
        self.op("dve", lambda e: e.reciprocal(out, in_), [in_], [out], r, w)

    def memset(self, out, val, eng="dve", r=None, w=None):
        self.op(eng, lambda e: e.memset(out, val), [], [out], r, w)


class _Stop(Exception):
    pass


def build(nseq, S, depth, dbg=False):
    import os
    stop_at = os.environ.get("KSTOP", "")

    def ck(name):
        if stop_at == name:
            raise _Stop()

    nc = bass.Bass("TRN2", target_bir_lowering=False)
    NT = S // T
    NB = S // 128
    ctab = chunk_table()

    def din(name, shape):
        return nc.dram_tensor(name, list(shape), F32, kind="ExternalInput").ap()

    x_d = din("x", [nseq, S, D])
    pr = {}
    for name, shape in [("norm_mix_g", [D]), ("w_in", [D, INW]), ("fox_forget_b", [FH]), ("fox_q_norm_g", [FD]),
                        ("fox_k_norm_g", [FD]), ("hgrn_lb_logits", [HH * HK]), ("hgrn_out_norm_g", [HV]),
                        ("conv_dw_w", [CK, CC]), ("conv_dw_b", [CC]), ("conv_norm_g", [CC]), ("conv_norm_b", [CC]),
                        ("gate_b", [3 * D]), ("w_branch", [D, D]), ("w_out", [D, D]), ("norm_ffn_g", [D]),
                        ("w_up", [D, 2 * DFF]), ("ffn_dw_w", [3, DFF]), ("ffn_dw_b", [DFF]), ("w_down", [DFF, D])]:
        pr[name] = din(name, [depth] + shape)
    cst_d = din("cstd", [128, 512])
    y_d = nc.dram_tensor("y", [nseq, S, D], F32, kind="ExternalOutput").ap()
    x1_d = nc.dram_tensor("x1s", [nseq, S, D], F32).ap()
    wsc = nc.dram_tensor("wsc", [depth * NCH, 128, 4096], BF16).ap()
    dbg_d = {}

    es = contextlib.ExitStack()
    with es:
        P = Prog(nc, es)

        def sb(name, shape, dt=F32):
            return es.enter_context(nc.sbuf_tensor(name, list(shape), dt))

        def psb(name, shape, dt=F32):
            return es.enter_context(nc.psum_tensor(name, list(shape), dt))

        cst = sb("cst", [128, 512])
        ident_b = sb("ident_b", [128, 128], BF16)
        tri_b = sb("tri_b", [128, 128], BF16)
        mask2_b = sb("mask2_b", [128, 128], BF16)
        ones_b = sb("ones_b", [128, 128], BF16)
        zeros_b = sb("zeros_b", [128, 392], BF16)
        identF, triF, mask2F, onesF = cst[:, 0:128], cst[:, 128:256], cst[:, 256:384], cst[:, 384:512]
        ones_bc = cst[:, 384:385].broadcast_to([128, T])

        ps = [psb(f"ps{i}", [128, 512]) for i in range(8)]
        psn = [0]

        def psum(pool):
            b = pool[psn[0] % len(pool)]
            psn[0] += 1
            return ps[b]

        tpool = [None]

        def psumT():
            b = psum(tpool[0] if tpool[0] is not None else ALLP)
            return b[:].bitcast(BF16)[:, 0:512]

        wr = [sb(f"wr{i}", [128, 8, 512], BF16) for i in range(NRING)]
        xt = sb("xt", [128, 4, D])
        hT = sb("hT", [128, 8, T], BF16)
        hpre = sb("hpre", [128, D], BF16)
        big = sb("big", [128, 24, T], BF16)
        NSCR = 6
        scr = [sb(f"scr{i}", [128, 520]) for i in range(NSCR)]
        scn = [0]

        def scratch():
            b = scr[scn[0] % NSCR]
            scn[0] += 1
            return b

        sqA = sb("sqA", [128, 1024])
        gmix = sb("gmix", [128, depth, 8])
        gffn = sb("gffn", [128, depth, 8])
        gateb = sb("gateb", [128, depth, 24])
        cvw = sb("cvw", [128, depth, 2, CK])
        cvb = sb("cvb", [128, depth, 2])
        cng = sb("cng", [128, depth, 2])
        cnb = sb("cnb", [128, depth, 2])
        fdw = sb("fdw", [128, depth, 3, NFF])
        fdb = sb("fdb", [128, depth, NFF])
        lbl = sb("lbl", [128, depth, HH])
        lb1 = sb("lb1", [128, depth, HH])
        oml = sb("oml", [128, depth, HH])
        fb_t = sb("fb_t", [128, depth, FH])
        gqk_t = sb("gqk_t", [128, 12, FD])
        gout_t = sb("gout_t", [128, HH, HV])
        st1 = sb("st1", [128, 16])
        st2 = sb("st2", [128, 16])
        st3 = sb("st3", [128, 16])
        KT = sb("KT", [128, 3, S], BF16)
        VA = sb("VA", [128, NB, FH, FD + 1], BF16)
        Cc = sb("Cc", [128, NB, FH])
        Srun = sb("Srun", [128, FH])
        lgf = sb("lgf", [128, FH])
        biasg = sb("biasg", [128, NB, FH])
        QT = sb("QT", [128, FH, T], BF16)
        qkn = sb("qkn", [128, 12, FD], BF16)
        PTs = [sb(f"PT{i}", [128, T], BF16) for i in range(3)]
        ptn2 = [0]
        ya_t = sb("ya_t", [128, FW], BF16)
        rc6 = sb("rc6", [128, FH])
        yT = sb("yT", [128, 8, T], BF16)
        hset = [big[:, 0:5, :], big[:, 5:10, :]]
        vtok = big[:, 10:13, :].rearrange("p a b -> p (a b)").rearrange("p (b v) -> p b v", v=HH * HV)
        sgg = big[:, 13:16, :].rearrange("p a b -> p (a b)").rearrange("p (b v) -> p b v", v=HH * HV)
        khtok = [big[:, 16 + 2 * i:18 + 2 * i, :].rearrange("p e (b k) -> p e b k", k=HK) for i in range(2)]
        Sst = sb("Sst", [128, HH, HV])
        Sbf = [big[:, 20 + 2 * i:22 + 2 * i, :].rearrange("p a b -> p (a b)")[:, 0:8 * HV].rearrange("p (c v) -> p c v", v=HV)
               for i in range(2)]
        Am = [sb(f"Am{i}", [128, 128], BF16) for i in range(2)]
        dch = sb("dch", [128, 8])
        yb_t = sb("yb_t", [128, HH * HV], BF16)
        t384 = sb("t384", [128, HH * HV])
        ubuf = sb("ubuf", [128, 2, 30 + T])
        cacc = sb("cacc", [128, 2, T])
        mT = hT
        gcar = sb("gcar", [128, NFF, 2])
        gbuf = [sb(f"gbuf{i}", [128, 2 + T]) for i in range(2)]

        print("sbuf bytes remaining/partition:", nc.sbuf_bytes_remaining)

        P.dma("sp", cst[:], cst_d, "ccst")
        P.copy(ident_b[:], identF)
        P.copy(tri_b[:], triF)
        P.copy(mask2_b[:], mask2F)
        P.copy(ones_b[:], onesF)
        P.memset(zeros_b[:], 0.0)
        P.memset(QT[:], 0.0)

        def load_pp(dst, vec, nchunk):
            for c0 in range(0, nchunk, 8):
                c1 = min(nchunk, c0 + 8)
                P.dma("sp", dst[:, c0:c1], vec[c0 * 128:c1 * 128].rearrange("(c p) -> p c", p=128), "c0",
                      allow_slow_non_contiguous=True)

        for l in range(depth):
            load_pp(gmix[:, l, :], pr["norm_mix_g"][l], 8)
            load_pp(gffn[:, l, :], pr["norm_ffn_g"][l], 8)
            load_pp(gateb[:, l, :], pr["gate_b"][l], 24)
            for j in range(CK):
                load_pp(cvw[:, l, :, j], pr["conv_dw_w"][l, j], 2)
            load_pp(cvb[:, l, :], pr["conv_dw_b"][l], 2)
            load_pp(cng[:, l, :], pr["conv_norm_g"][l], 2)
            load_pp(cnb[:, l, :], pr["conv_norm_b"][l], 2)
            for j in range(3):
                load_pp(fdw[:, l, j, :], pr["ffn_dw_w"][l, j], NFF)
            load_pp(fdb[:, l, :], pr["ffn_dw_b"][l], NFF)
            load_pp(lbl[:, l, :], pr["hgrn_lb_logits"][l], HH)
            P.dma("sp", fb_t[:, l, :], pr["fox_forget_b"][l].partition_broadcast(128), "c0")

        for nm in ["gmix", "gffn", "gateb", "cvw", "cvb", "cng", "cnb", "fdw", "fdb", "lbl", "fb_t"]:
            P.lw[nm] = (("d", "c0"), P.dcnt["c0"])
        esum = st1[:, 0:HH]
        etmp = sb("etmp", [128, depth, HH])
        P.act(etmp[:], lbl[:], AF.Exp)
        P.copy(esum, etmp[:, 0, :])
        for l in range(1, depth):
            P.tt(esum, esum, etmp[:, l, :], ALU.add)
        P.recip(esum, esum)
        P.memset(lb1[:, 0, :], 0.0)
        for l in range(1, depth):
            P.tt(st2[:, 0:HH], etmp[:, l, :], esum, ALU.mult)
            P.tt(lb1[:, l, :], lb1[:, l - 1, :], st2[:, 0:HH], ALU.add)
        for l in range(depth):
            P.ts(lb1[:, l, :], lb1[:, l, :], 0.0, None, ALU.max)
            P.ts(oml[:, l, :], lb1[:, l, :], -1.0, 1.0, ALU.mult, ALU.add)

        for l in range(depth):
            for ci, (mat, row0, nkc, parts) in enumerate(ctab):
                W = pr[mat][l]
                dst = wsc[l * NCH + ci].rearrange("p (kc n) -> p kc n", n=512)
                for (c0, ncol, d0) in parts:
                    src = W[row0:row0 + nkc * 128, c0:c0 + ncol].rearrange("(kc p) n -> p kc n", p=128)
                    P.dma("pool", dst[:, 0:nkc, d0:d0 + ncol], src, f"wprep{l}", r=[], w=[f"wsc{l}"])

        useq = [(l, ci) for l in range(depth) for s in range(nseq) for t in range(NT) for ci in ORDER]
        wst = {"issued": 0, "used": 0, "done": 0}

        def wpump():
            while wst["issued"] < len(useq) and wst["issued"] < wst["done"] + NRING:
                m = wst["issued"]
                ll, cc = useq[m]
                slot = m % NRING
                nkc = ctab[cc][2]
                ncl = max(d0 + ncol for (_, ncol, d0) in ctab[cc][3])
                src = wsc[ll * NCH + cc].rearrange("p (kc n) -> p kc n", n=512)
                P.dma("sp", wr[slot][:, 0:nkc, 0:ncl], src[:, 0:nkc, 0:ncl], f"wr{slot}", r=[f"wsc{ll}"], w=[f"wr{slot}"])
                wst["issued"] += 1

        def wget(l, ci):
            n = wst["used"]
            assert useq[n] == (l, ci), (useq[n], l, ci)
            wpump()
            assert wst["issued"] > n, "weight ring too small for the number of chunks held"
            wst["used"] += 1
            return wr[n % NRING]

        def wrel(k=1):
            wst["done"] += k
            assert wst["done"] <= wst["used"]
            wpump()

        def rmsnorm_to_hT(l, gt):
            for b in range(4):
                P.act(sqA[:], xt[:, b, :], AF.Square)
                P.red(st1[:, b:b + 1], sqA[:])
            P.act(st2[:, 0:4], st1[:, 0:4], AF.Sqrt, bias=EPS, scale=1.0 / D)
            P.recip(st3[:, 0:4], st2[:, 0:4])
            for b in range(4):
                hp = sb_hp[b % 2]
                P.ts(hp[:], xt[:, b, :], st3[:, b:b + 1], None, ALU.mult)
                for grp in range(2):
                    pt = psumT()
                    for k4 in range(4):
                        kc = grp * 4 + k4
                        P.tr(pt[:, k4 * 128:(k4 + 1) * 128], hp[:, kc * 128:(kc + 1) * 128], ident_b[:])
                    if grp == 0:
                        P.tt(hT[:, 0:4, b * 128:(b + 1) * 128], pt.rearrange("p (k t) -> p k t", t=128),
                             gt[:, l, 0:4].unsqueeze(2).broadcast_to([128, 4, 128]), ALU.mult)
                    else:
                        for k4 in range(4):
                            kc = 4 + k4
                            P.act(hT[:, kc, b * 128:(b + 1) * 128], pt[:, k4 * 128:(k4 + 1) * 128], AF.Copy,
                                  scale=gt[:, l, kc:kc + 1])

        sb_hp = [hpre, sb("hpre1", [128, D], BF16)]

        def proj_fm(wt, sub, out_ps, n=T):
            for kc in range(8):
                P.mm(out_ps[:, 0:n], wt[:, kc, sub * 128:(sub + 1) * 128], hT[:, kc, 0:n], start=(kc == 0), stop=(kc == 7))

        def proj_tm(wt, ncol, b, out_ps):
            for kc in range(8):
                P.mm(out_ps[:, 0:ncol], hT[:, kc, b * 128:(b + 1) * 128], wt[:, kc, 0:ncol], start=(kc == 0), stop=(kc == 7))

        PB = [4, 5, 6]
        ALLP = [0, 1, 2, 3, 4, 5, 6, 7]

        out_cnt = [0]

        try:
            for l in range(depth):
                src_d = x_d if l == 0 else x1_d
                for j in range(6):
                    P.dma("sp", gqk_t[:, j, :], pr["fox_q_norm_g"][l].partition_broadcast(128), f"cl{l}")
                    P.dma("sp", gqk_t[:, 6 + j, :], pr["fox_k_norm_g"][l].partition_broadcast(128), f"cl{l}")
                for j in range(HH):
                    P.dma("sp", gout_t[:, j, :], pr["hgrn_out_norm_g"][l].partition_broadcast(128), f"cl{l}")
                for nm in ["gqk_t", "gout_t"]:
                    P.lw[nm] = (("d", f"cl{l}"), P.dcnt[f"cl{l}"])
                dst_d = y_d if l == depth - 1 else x1_d
                for s in range(nseq):
                    P.memset(VA[:, :, :, FD:FD + 1], 1.0)
                    P.memset(Srun[:], 0.0)
                    P.memset(Sst[:], 0.0)
                    P.memset(ubuf[:, :, 0:30], 0.0)
                    P.memset(gcar[:], 0.0)
                    for g in range(NT):
                        tok0 = g * T
                        ck("prepass")
                        xsrc = src_d[s, tok0:tok0 + T, :].rearrange("(b p) d -> p b d", p=128)
                        rdeps = ["x1s"] if l > 0 else []
                        P.dma("sp", xt[:], xsrc, "xin", r=rdeps, w=["xt"])
                        rmsnorm_to_hT(l, gmix)

                        ck("norm")
                        wA = [wget(l, 0), wget(l, 1), wget(l, 2)]
                        for b in range(4):
                            blk = g * 4 + b
                            p0, p1, p2 = psum(ALLP), psum(ALLP), psum(ALLP)
                            proj_tm(wA[0], 512, b, p0)
                            proj_tm(wA[1], 512, b, p1)
                            proj_tm(wA[2], 134, b, p2)
                            P.act(sqA[:, 0:512], p0[:], AF.Square)
                            P.act(sqA[:, 512:768], p1[:, 0:256], AF.Square)
                            P.red(st1[:, 0:12], sqA[:, 0:768].rearrange("p (h d) -> p h d", d=FD))
                            P.act(st2[:, 0:12], st1[:, 0:12], AF.Sqrt, bias=EPS, scale=1.0 / FD)
                            P.recip(st3[:, 0:12], st2[:, 0:12])
                            P.tt(sqA[:, 0:512].rearrange("p (h d) -> p h d", d=FD), p0[:].rearrange("p (h d) -> p h d", d=FD),
                                 st3[:, 0:8].unsqueeze(2).broadcast_to([128, 8, FD]), ALU.mult)
                            P.tt(sqA[:, 512:768].rearrange("p (h d) -> p h d", d=FD), p1[:, 0:256].rearrange("p (h d) -> p h d", d=FD),
                                 st3[:, 8:12].unsqueeze(2).broadcast_to([128, 4, FD]), ALU.mult)
                            P.tt(qkn[:], sqA[:, 0:768].rearrange("p (h d) -> p h d", d=FD), gqk_t[:, :, :], ALU.mult)
                            qv = qkn[:].rearrange("p h d -> p (h d)")
                            pt = psumT()
                            for pp in range(3):
                                P.tr(pt[:, pp * 128:(pp + 1) * 128], qv[:, pp * 128:(pp + 1) * 128], ident_b[:])
                            ptv = pt[:, 0:384].rearrange("p (c t) -> p c t", t=128)
                            P.copy(QT[0:64, 0:FH:2, b * 128:(b + 1) * 128], ptv[0:64], eng="act")
                            P.copy(QT[64:128, 1:FH:2, b * 128:(b + 1) * 128], ptv[64:128])
                            pt = psumT()
                            for pp in range(3):
                                P.tr(pt[:, pp * 128:(pp + 1) * 128], qv[:, 384 + pp * 128:384 + (pp + 1) * 128], ident_b[:])
                            P.copy(KT[:, :, tok0 + b * 128:tok0 + (b + 1) * 128], pt[:, 0:384].rearrange("p (c t) -> p c t", t=128))
                            P.copy(VA[:, blk, 0:4, 0:FD], p1[:, 256:512].rearrange("p (h d) -> p h d", d=FD), eng="act")
                            P.copy(VA[:, blk, 4:6, 0:FD], p2[:, 0:128].rearrange("p (h d) -> p h d", d=FD))
                            P.tt(lgf[:], p2[:, 128:134], fb_t[:, l, :], ALU.add)
                            P.act(lgf[:], lgf[:], AF.Exp, scale=-1.0)
                            P.act(lgf[:], lgf[:], AF.Ln, bias=1.0)
                            P.ts(lgf[:], lgf[:], -1.0, None, ALU.mult)
                            pc = psum(ALLP)
                            P.mm(pc[:, 0:FH], triF, lgf[:], start=True, stop=False)
                            P.mm(pc[:, 0:FH], onesF, Srun[:], start=False, stop=True)
                            P.copy(Cc[:, blk, :], pc[:, 0:FH])
                            P.tt(Srun[:], Srun[:], lgf[:], ALU.add)
                        wrel(3)
                        pc = psum(ALLP)
                        P.mm(pc[:, 0:FH], onesF, Srun[:])
                        P.copy(st1[:, 0:FH], pc[:, 0:FH])
                        nkb = g * 4 + 4
                        P.tt(biasg[:, 0:nkb, :], st1[:, 0:FH].unsqueeze(1).broadcast_to([128, nkb, FH]), Cc[:, 0:nkb, :], ALU.subtract)

                        ck("groupA")
                        oacc = [ps[i] for i in range(4)]
                        tpool[0] = [7]
                        for qb in range(4):
                            P.mm(oacc[qb][:, 0:FH * 65], zeros_b[:, 0:128], zeros_b[:, 0:FH * 65], start=True, stop=False, skip=True)
                        for h in range(FH):
                            e, pp = h % 2, h // 2
                            rows = slice(e * 64, (e + 1) * 64)
                            for j in range(nkb):
                                jb = j - g * 4
                                qlo = max(0, jb) * 128
                                pS = ps[4 + (psn[0] % 3)]
                                psn[0] += 1
                                P.mm(pS[:, qlo:T], KT[:, pp, j * 128:(j + 1) * 128], QT[:, h, qlo:T])
                                PT = PTs[ptn2[0] % 3]
                                ptn2[0] += 1
                                P.act(PT[:, qlo:T], pS[:, qlo:T], AF.Exp, bias=biasg[:, j, h:h + 1], scale=FD ** -0.5)
                                if jb >= 0:
                                    P.tt(PT[:, qlo:qlo + 128], PT[:, qlo:qlo + 128], tri_b[:], ALU.mult, eng="dve")
                                for qb in range(max(0, jb), 4):
                                    P.mm(oacc[qb][:, h * 65:(h + 1) * 65], PT[:, qb * 128:(qb + 1) * 128], VA[:, j, h, :],
                                         start=False, stop=(h == FH - 1 and j == g * 4 + qb), skip=True)
                        for qb in range(4):
                            ov = oacc[qb][:, 0:FH * 65].rearrange("p (h d) -> p h d", d=65)
                            P.recip(rc6[:], ov[:, :, 64])
                            P.tt(ya_t[:].rearrange("p (h d) -> p h d", d=FD), ov[:, :, 0:FD],
                                 rc6[:].unsqueeze(2).broadcast_to([128, FH, FD]), ALU.mult)
                            pt = psumT()
                            for pp in range(3):
                                P.tr(pt[:, pp * 128:(pp + 1) * 128], ya_t[:, pp * 128:(pp + 1) * 128], ident_b[:])
                            P.copy(yT[:, 0:3, qb * 128:(qb + 1) * 128], pt[:, 0:384].rearrange("p (c t) -> p c t", t=128), eng="act")

                        ck("attn")
                        wHI = wget(l, CH_HI)
                        for b in range(4):
                            p0 = psum(PB)
                            proj_tm(wHI, 384, b, p0)
                            P.copy(vtok[:, b, :], p0[:, 0:384], eng="act")
                        wrel()
                        wHG = wget(l, CH_HG)
                        for b in range(4):
                            p0 = psum(PB)
                            proj_tm(wHG, 384, b, p0)
                            P.act(sgg[:, b, :], p0[:, 0:384], AF.Silu)
                            P.tt(sgg[:, b, :], sgg[:, b, :], gout_t[:, :, :].rearrange("p h v -> p (h v)"), ALU.mult, eng="dve")
                        wrel()
                        wHQ = wget(l, CH_HQ)
                        wHF = wget(l, CH_HF)
                        ob = [ps[i] for i in range(4)]
                        for hh in range(HH):
                            scn[0] = 0
                            hs = hset[hh % 2]
                            kht = khtok[hh % 2]
                            sbf = Sbf[hh % 2]
                            pq = psum(PB)
                            proj_fm(wHQ, hh, pq)
                            pf = psum(PB)
                            proj_fm(wHF, hh, pf)
                            qf = scratch()
                            P.copy(qf[:, 0:T], pq[:], eng="act")
                            fg = scratch()
                            P.act(fg[:, 0:T], pf[:], AF.Sigmoid)
                            P.ts(fg[:, 0:T], fg[:, 0:T], oml[:, l, hh:hh + 1], lb1[:, l, hh:hh + 1], ALU.mult, ALU.add)
                            kk = scratch()
                            P.ts(kk[:, 0:T], fg[:, 0:T], -1.0, 1.0, ALU.mult, ALU.add)
                            P.act(fg[:, 0:T], fg[:, 0:T], AF.Ln)
                            G = scratch()
                            P.memset(G[:, 0:1], 0.0)
                            P.op("dve", lambda e_: e_.tensor_tensor_scan(G[:, 1:T + 1], ones_bc, fg[:, 0:T], 0.0, ALU.mult, ALU.add),
                                 [ones_bc, fg[:, 0:T]], [G[:, 1:T + 1]])
                            Gv = G[:, 1:T + 1].rearrange("p (c t) -> p c t", t=64)
                            s_c = G[:, 0:T:64]
                            r_c = G[:, 32:T:64]
                            e_c = G[:, 64:T + 1:64]

                            def bc(v):
                                return v.unsqueeze(2).broadcast_to([128, 8, 64])

                            Ea, Xa = scratch(), scratch()
                            Eb, Xb = Ea, Xa

                            def v3(t_):
                                return t_[:, 0:T].rearrange("p (c t) -> p c t", t=64)

                            P.tt(v3(Ea), Gv, bc(r_c), ALU.subtract)
                            P.act(Xa[:, 0:T], Ea[:, 0:T], AF.Exp)
                            P.tt(hs[:, 0, :], qf[:, 0:T], Xa[:, 0:T], ALU.mult)
                            P.act(Xb[:, 0:T], Ea[:, 0:T], AF.Exp, scale=-1.0)
                            P.tt(hs[:, 1, :], kk[:, 0:T], Xb[:, 0:T], ALU.mult)
                            P.tt(v3(Eb), Gv, bc(s_c), ALU.subtract)
                            P.act(Xa[:, 0:T], Eb[:, 0:T], AF.Exp)
                            def v4(t_):
                                return t_.rearrange("p (b e t) -> p b e t", e=2, t=64)

                            for e in range(2):
                                P.tt(v4(hs[:, 2 + e, :])[:, :, e, :], v4(qf[:, 0:T])[:, :, e, :], v4(Xa[:, 0:T])[:, :, e, :], ALU.mult)
                                P.memset(v4(hs[:, 2 + e, :])[:, :, 1 - e, :], 0.0)
                            P.tt(v3(Ea), bc(e_c), Gv, ALU.subtract)
                            P.act(Xb[:, 0:T], Ea[:, 0:T], AF.Exp)
                            P.tt(hs[:, 4, :], kk[:, 0:T], Xb[:, 0:T], ALU.mult)
                            P.tt(dch[:], e_c, s_c, ALU.subtract)
                            P.act(dch[:], dch[:], AF.Exp)
                            pt = psumT()
                            for b in range(4):
                                P.tr(pt[:, b * 128:(b + 1) * 128], hs[:, 4, b * 128:(b + 1) * 128], ident_b[:])
                            P.copy(kht[0:64, 0].rearrange("p b k -> p (b k)"), pt[0:64, :], eng="act")
                            P.copy(kht[64:128, 1].rearrange("p b k -> p (b k)"), pt[64:128, :])
                            P.memset(kht[64:128, 0], 0.0)
                            P.memset(kht[0:64, 1], 0.0)
                            pU = [psum(PB), psum(PB)]
                            for c in range(8):
                                b, e = c // 2, c % 2
                                rows = slice(e * 64, (e + 1) * 64)
                                P.copy(sbf[:, c, :], Sst[:, hh, :], eng="act")
                                pu = pU[c // 4][:, (c % 4) * HV:(c % 4 + 1) * HV]
                                P.mm(pu, kht[:, e, b, :], vtok[:, b, hh * HV:(hh + 1) * HV])
                                P.stt(Sst[:, hh, :], Sst[:, hh, :], dch[:, c:c + 1], pu, ALU.mult, ALU.add)
                            for b in range(4):
                                cols = slice(b * 128, (b + 1) * 128)
                                pa = psum(PB)
                                P.mm(pa[:, 0:128], hs[:, 1, cols], hs[:, 0, cols])
                                am = Am[b % 2]
                                P.tt(am[:], pa[:, 0:128], mask2_b[:], ALU.mult)
                                oslc = ob[b][:, hh * HV:(hh + 1) * HV]
                                P.mm(oslc, am[:], vtok[:, b, hh * HV:(hh + 1) * HV], start=True, stop=False)
                                for e in range(2):
                                    c = 2 * b + e
                                    P.mm(oslc, hs[:, 2 + e, cols], sbf[:, c, :], start=False, stop=(e == 1))
                        wrel(2)
                        for b in range(4):
                            o3 = ob[b][:, 0:HH * HV]
                            P.act(t384[:], o3, AF.Square)
                            P.red(st1[:, 0:HH], t384[:].rearrange("p (h v) -> p h v", v=HV))
                            P.act(st2[:, 0:HH], st1[:, 0:HH], AF.Sqrt, bias=EPS, scale=1.0 / HV)
                            P.recip(st3[:, 0:HH], st2[:, 0:HH])
                            P.tt(t384[:].rearrange("p (h v) -> p h v", v=HV), o3.rearrange("p (h v) -> p h v", v=HV),
                                 st3[:, 0:HH].unsqueeze(2).broadcast_to([128, HH, HV]), ALU.mult)
                            P.tt(yb_t[:], t384[:], sgg[:, b, :], ALU.mult)
                            pt = psumT()
                            for pp in range(3):
                                P.tr(pt[:, pp * 128:(pp + 1) * 128], yb_t[:, pp * 128:(pp + 1) * 128], ident_b[:])
                            P.copy(yT[:, 3:6, b * 128:(b + 1) * 128], pt[:, 0:384].rearrange("p (c t) -> p c t", t=128), eng="act")

                        tpool[0] = None
                        ck("hgrn")
                        wC = wget(l, CH_C)
                        scn[0] = 0
                        for c in range(2):
                            pa = psum(ALLP)
                            proj_fm(wC, c, pa)
                            pb_ = psum(ALLP)
                            proj_fm(wC, 2 + c, pb_)
                            sg = scratch()
                            P.act(sg[:, 0:T], pb_[:], AF.Sigmoid)
                            P.tt(ubuf[:, c, 30:30 + T], pa[:], sg[:, 0:T], ALU.mult)
                            P.ts(cacc[:, c, :], ubuf[:, c, 0:T], cvw[:, l, c, 0:1], cvb[:, l, c:c + 1], ALU.mult, ALU.add)
                            for j in range(1, CK):
                                P.stt(cacc[:, c, :], ubuf[:, c, j:j + T], cvw[:, l, c, j:j + 1], cacc[:, c, :], ALU.mult, ALU.add)
                            P.copy(ubuf[:, c, 0:30], ubuf[:, c, T:T + 30], eng="dve")
                        wrel()
                        pm = psum(ALLP)
                        pq2 = psum(ALLP)
                        for c in range(2):
                            P.act(big[:, c, :], cacc[:, c, :], AF.Copy)
                            P.act(big[:, 2 + c, :], cacc[:, c, :], AF.Square)
                        for c in range(2):
                            P.mm(pm[:], ones_b[:], big[:, c, :], start=(c == 0), stop=(c == 1))
                        for c in range(2):
                            P.mm(pq2[:], ones_b[:], big[:, 2 + c, :], start=(c == 0), stop=(c == 1))
                        mean = scratch()
                        P.act(mean[:, 0:T], pm[:], AF.Copy, scale=1.0 / CC)
                        var = scratch()
                        P.tt(var[:, 0:T], mean[:, 0:T], mean[:, 0:T], ALU.mult)
                        P.stt(var[:, 0:T], pq2[:], 1.0 / CC, var[:, 0:T], ALU.mult, ALU.subtract)
                        P.act(var[:, 0:T], var[:, 0:T], AF.Sqrt, bias=EPS)
                        P.recip(var[:, 0:T], var[:, 0:T])
                        for c in range(2):
                            dd = scratch()
                            P.tt(dd[:, 0:T], cacc[:, c, :], mean[:, 0:T], ALU.subtract)
                            P.tt(dd[:, 0:T], dd[:, 0:T], var[:, 0:T], ALU.mult)
                            P.act(yT[:, 6 + c, :], dd[:, 0:T], AF.Silu, bias=cnb[:, l, c:c + 1], scale=cng[:, l, c:c + 1])

                        ck("conv")
                        for i in range(6):
                            wG = wget(l, CH_G + i)
                            for sub in range(4):
                                n = i * 4 + sub
                                pg = psum(ALLP)
                                proj_fm(wG, sub, pg)
                                P.act(big[:, n, :], pg[:], AF.Sigmoid, bias=gateb[:, l, n:n + 1])
                            wrel()

                        ck("gates")
                        kcs = [(0, 3), (3, 6), (6, 8)]
                        for i in range(2):
                            wB = wget(l, CH_BR + i)
                            for sub in range(4):
                                dc = i * 4 + sub
                                pj = []
                                for j in range(3):
                                    pp_ = psum(ALLP)
                                    k0, k1 = kcs[j]
                                    for kc in range(k0, k1):
                                        P.mm(pp_[:], wB[:, kc, sub * 128:(sub + 1) * 128], yT[:, kc, :], start=(kc == k0), stop=(kc == k1 - 1))
                                    pj.append(pp_)
                                m0 = scratch()
                                m1 = scratch()
                                P.tt(m0[:, 0:T], pj[0][:], big[:, dc, :], ALU.mult)
                                P.tt(m1[:, 0:T], pj[1][:], big[:, 8 + dc, :], ALU.mult)
                                P.tt(m0[:, 0:T], m0[:, 0:T], m1[:, 0:T], ALU.add, eng="dve")
                                m2 = scratch()
                                P.tt(m2[:, 0:T], pj[2][:], big[:, 16 + dc, :], ALU.mult)
                                P.tt(mT[:, dc, :], m0[:, 0:T], m2[:, 0:T], ALU.add, eng="dve")
                            wrel()

                        ck("merge")
                        wO = [wget(l, CH_OUT), wget(l, CH_OUT + 1)]
                        for b in range(4):
                            for hf in range(2):
                                po = psum(ALLP)
                                for kc in range(8):
                                    P.mm(po[:], mT[:, kc, b * 128:(b + 1) * 128], wO[hf][:, kc, :], start=(kc == 0), stop=(kc == 7))
                                P.tt(xt[:, b, hf * 512:(hf + 1) * 512], xt[:, b, hf * 512:(hf + 1) * 512], po[:], ALU.add)
                        wrel(2)

                        ck("outproj")
                        rmsnorm_to_hT(l, gffn)
                        for i in range(11):
                            wU = wget(l, CH_UP + i)
                            for u in range(2):
                                fc = 2 * i + u
                                pg = psum(ALLP)
                                proj_fm(wU, u, pg)
                                pv = psum(ALLP)
                                proj_fm(wU, 2 + u, pv)
                                gb = gbuf[fc % 2]
                                P.copy(gb[:, 0:2], gcar[:, fc, :], eng="dve")
                                P.copy(gb[:, 2:2 + T], pg[:], eng="act")
                                P.copy(gcar[:, fc, :], gb[:, T:T + 2], eng="dve")
                                a = scratch()
                                P.ts(a[:, 0:T], gb[:, 2:2 + T], fdw[:, l, 2, fc:fc + 1], fdb[:, l, fc:fc + 1], ALU.mult, ALU.add)
                                P.stt(a[:, 0:T], gb[:, 1:1 + T], fdw[:, l, 1, fc:fc + 1], a[:, 0:T], ALU.mult, ALU.add)
                                P.stt(a[:, 0:T], gb[:, 0:T], fdw[:, l, 0, fc:fc + 1], a[:, 0:T], ALU.mult, ALU.add)
                                P.act(a[:, 0:T], a[:, 0:T], AF.Silu)
                                P.tt(big[:, fc, :], a[:, 0:T], pv[:], ALU.mult)
                            wrel()
                        for hf in range(2):
                            acc = [ps[i] for i in range(4)]
                            for kg in range(3):
                                wD = wget(l, CH_DN + hf * 3 + kg)
                                nk = 8 if kg < 2 else 6
                                for b in range(4):
                                    for kc in range(nk):
                                        fc = kg * 8 + kc
                                        P.mm(acc[b][:], big[:, fc, b * 128:(b + 1) * 128], wD[:, kc, :],
                                             start=(fc == 0), stop=(fc == NFF - 1))
                                wrel()
                            for b in range(4):
                                P.tt(xt[:, b, hf * 512:(hf + 1) * 512], xt[:, b, hf * 512:(hf + 1) * 512], acc[b][:], ALU.add)
                        ck("ffn")
                        ydst = dst_d[s, tok0:tok0 + T, :].rearrange("(b p) d -> p b d", p=128)
                        if l == depth - 1:
                            P.dma("sp", ydst, xt[:], "yout", r=["xt"], w=[])
                            out_cnt[0] += 16
                        else:
                            P.dma("sp", ydst, xt[:], "x1w", r=["xt"], w=["x1s"])
        except _Stop:
            for e_ in ("pe", "act", "dve", "pool"):
                if P.cnt[e_] > 0:
                    nc.sync.wait_ge(P.sem[e_], P.cnt[e_])
            P.dma("sp", y_d[0, 0:T, :].rearrange("(b p) d -> p b d", p=128), xt[:], "yout", r=["xt"], w=[])
            out_cnt[0] += 16
        nc.sync.wait_ge(P.dsem["yout"], out_cnt[0])
        print("instr counts", P.cnt, "waits", P.nwaits, "dma", {k: v // 16 for k, v in P.dcnt.items()})
    return nc


def make_consts():
    c = np.zeros((128, 512), np.float32)
    i = np.arange(128)
    c[:, 0:128] = np.eye(128, dtype=np.float32)
    tri = (i[:, None] <= i[None, :]).astype(np.float32)
    c[:, 128:256] = tri
    same = (i[:, None] // 64) == (i[None, :] // 64)
    c[:, 256:384] = tri * same
    c[:, 384:512] = 1.0
    return c


PARAM_NAMES = ["norm_mix_g", "w_in", "fox_forget_b", "fox_q_norm_g", "fox_k_norm_g", "hgrn_lb_logits",
               "hgrn_out_norm_g", "conv_dw_w", "conv_dw_b", "conv_norm_g", "conv_norm_b", "gate_b", "w_branch",
               "w_out", "norm_ffn_g", "w_up", "ffn_dw_w", "ffn_dw_b", "w_down"]


def run(inputs, ncores):
    x = np.ascontiguousarray(np.asarray(inputs["x"], dtype=np.float32))
    B, S, _ = x.shape
    depth = int(np.asarray(inputs["w_in"]).shape[0])
    nseq = B // ncores
    nc = build(nseq, S, depth)
    cst = make_consts()
    params = {k: np.ascontiguousarray(np.asarray(inputs[k], dtype=np.float32)) for k in PARAM_NAMES}
    in_maps = []
    for c in range(ncores):
        m = {"x": x[c * nseq:(c + 1) * nseq], "cstd": cst}
        m.update(params)
        in_maps.append(m)
    import os, time
    t0 = time.time()
    if os.environ.get("KTRACE"):
        res = run_bass_kernel_spmd(nc, in_maps, core_ids=list(range(ncores)), trace=True)
        print("exec_time_ns", res.exec_time_ns)
    else:
        res = run_bass_kernel_spmd(nc, in_maps, core_ids=list(range(ncores)))
    print("run_bass_kernel_spmd wall", time.time() - t0)
    return np.concatenate([np.asarray(r["y"]) for r in res.results], axis=0).astype(np.float32)


def kernel(**inputs):
    return run(inputs, 8)
```
